# Optimizing a Trainium2 kernel written in Bass

```python
import math, functools
import jax, jax.numpy as jnp
from jax import lax
import numpy as np

D_MODEL = 1024
BATCH = 2
SEQ = 8192
DEPTH = 2

ATT_HEADS = 8
ATT_HEAD_DIM = 64
ATT_WIDTH = ATT_HEADS * ATT_HEAD_DIM
ATT_PATTERNS = ((128, 1), (512, 4), (2048, 16))
ATT_BLOCK = 128
SSD_HEADS = 8
SSD_HEAD_DIM = 64
SSD_WIDTH = SSD_HEADS * SSD_HEAD_DIM
SSD_GROUPS = 2
SSD_STATE = 128
SSD_CHUNK = 128
SSD_CONV_CH = SSD_WIDTH + 2 * SSD_GROUPS * SSD_STATE
DT_MIN = 0.001
DT_MAX = 0.1
LRU_WIDTH = 512
LRU_BLOCKS = 8
LRU_BLOCK_W = LRU_WIDTH // LRU_BLOCKS
LRU_C = 8.0
CONV_WIDTH = 4
D_MIX = ATT_WIDTH + SSD_WIDTH + LRU_WIDTH
IN_COLS = 3 * ATT_WIDTH + SSD_WIDTH + SSD_CONV_CH + SSD_HEADS + 2 * LRU_WIDTH
D_FF = ((8 * D_MODEL + 3 * 256 - 1) // (3 * 256)) * 256
NORM_EPS = 1e-6
SSD_NORM_EPS = 1e-5

kernel_name = 'hybrid_dilated_attn_ssd_rglru_block'


def rmsnorm(x, g, eps=NORM_EPS):
    xf = x.astype(jnp.float32)
    y = xf * lax.rsqrt(jnp.mean(xf * xf, axis=-1, keepdims=True) + eps)
    return (y * g.astype(jnp.float32)).astype(x.dtype)


def causal_depthwise_conv(x, w, b):
    k_width, s = w.shape[0], x.shape[1]
    xp = jnp.pad(x, ((0, 0), (k_width - 1, 0), (0, 0)))
    y = b + xp[:, k_width - 1:] * w[k_width - 1]
    for k in range(k_width - 1):
        y = y + xp[:, k:k + s] * w[k]
    return y


def split_cols(proj):
    sizes = (ATT_WIDTH, ATT_WIDTH, ATT_WIDTH, SSD_WIDTH, SSD_CONV_CH, SSD_HEADS, LRU_WIDTH, LRU_WIDTH)
    offsets = np.cumsum(sizes)[:-1].tolist()
    return jnp.split(proj, offsets, axis=-1)


def dilated_branch(q, k, v, window, dilation, slopes):
    b, s, h, dh = q.shape
    length = s // dilation
    nb = -(-length // ATT_BLOCK)
    pad = nb * ATT_BLOCK - length

    def to_blocks(t):
        t = t.reshape(b, length, dilation, h, dh).transpose(0, 2, 3, 1, 4)
        t = jnp.pad(t, ((0, 0), (0, 0), (0, 0), (0, pad), (0, 0)))
        return t.reshape(b, dilation, h, nb, ATT_BLOCK, dh)

    def with_prev(t):
        prev = jnp.pad(t[:, :, :, :-1], ((0, 0), (0, 0), (0, 0), (1, 0), (0, 0), (0, 0)))
        return jnp.concatenate([prev, t], axis=4)

    qb, kb, vb = to_blocks(q), to_blocks(k), to_blocks(v)
    kk, vv = with_prev(kb), with_prev(vb)
    scores = jnp.einsum('brhnqd,brhnkd->brhnqk', qb, kk,
                        preferred_element_type=jnp.float32) * (ATT_HEAD_DIM ** -0.5)
    qi = jnp.arange(ATT_BLOCK)[:, None]
    ki = jnp.arange(2 * ATT_BLOCK)[None, :]
    dist = ATT_BLOCK + qi - ki
    band = (dist >= 0) & (dist <= window // dilation)
    valid = band[None] & ((jnp.arange(nb)[:, None, None] > 0) | (ki[None] >= ATT_BLOCK))
    alibi = -slopes[:, None, None] * (dilation * dist).astype(jnp.float32)
    scores = scores + alibi[None, None, :, None]
    scores = jnp.where(valid[None, None, None], scores, -jnp.inf)
    m = jnp.max(scores, axis=-1)
    p = jnp.exp(scores - m[..., None])
    den = jnp.sum(p, axis=-1)
    num = jnp.einsum('brhnqk,brhnkd->brhnqd', p, vv.astype(jnp.float32))

    def from_blocks(t):
        t = t.reshape(b, dilation, h, nb * ATT_BLOCK, *t.shape[5:])[:, :, :, :length]
        t = jnp.moveaxis(t, 3, 1)
        return t.reshape(b, s, h, *t.shape[4:])

    return from_blocks(num), from_blocks(m), from_blocks(den)


def dilated_attention(q, k, v):
    b, s, _ = q.shape
    shp = (b, s, ATT_HEADS, ATT_HEAD_DIM)
    q, k, v = q.reshape(shp), k.reshape(shp), v.reshape(shp)
    slopes = jnp.exp2(-8.0 * jnp.arange(1, ATT_HEADS + 1, dtype=jnp.float32) / ATT_HEADS)
    branches = [dilated_branch(q, k, v, w, d, slopes) for (w, d) in ATT_PATTERNS]
    m_all = functools.reduce(jnp.maximum, [br[1] for br in branches])
    num = jnp.zeros(shp, jnp.float32)
    den = jnp.zeros(shp[:3], jnp.float32)
    for n_g, m_g, d_g in branches:
        e = jnp.exp(m_g - m_all)
        num = num + n_g * e[..., None]
        den = den + d_g * e
    return (num / den[..., None]).reshape(b, s, ATT_WIDTH)


def ssd_chunked_scan(x, dt, a, bm, cm):
    b, s, h, p = x.shape
    n = bm.shape[-1]
    q = SSD_CHUNK
    nc = s // q
    rep = h // bm.shape[2]
    x = x.reshape(b, nc, q, h, p)
    dt = dt.reshape(b, nc, q, h)
    bh = jnp.repeat(bm, rep, axis=2).reshape(b, nc, q, h, n)
    ch = jnp.repeat(cm, rep, axis=2).reshape(b, nc, q, h, n)
    acs = jnp.cumsum(dt * a, axis=2)
    seg = acs[:, :, :, None, :] - acs[:, :, None, :, :]
    causal = jnp.tril(jnp.ones((q, q), dtype=bool))[None, None, :, :, None]
    lmat = jnp.exp(jnp.where(causal, seg, -jnp.inf))
    scores = jnp.einsum('bcihn,bcjhn->bcijh', ch, bh) * lmat * dt[:, :, None, :, :]
    y_diag = jnp.einsum('bcijh,bcjhp->bcihp', scores, x)
    decay_to_end = jnp.exp(acs[:, :, -1:, :] - acs)
    states = jnp.einsum('bcjhn,bcjh,bcjhp->bchpn', bh, decay_to_end * dt, x)
    chunk_decay = jnp.exp(acs[:, :, -1, :])

    def step(carry, inp):
        st, dc = inp
        return dc[:, :, None, None] * carry + st, carry

    _, prev = lax.scan(step, jnp.zeros((b, h, p, n), x.dtype),
                       (jnp.moveaxis(states, 1, 0), jnp.moveaxis(chunk_decay, 1, 0)))
    prev = jnp.moveaxis(prev, 0, 1)
    y_off = jnp.einsum('bcihn,bchpn,bcih->bcihp', ch, prev, jnp.exp(acs))
    return (y_diag + y_off).reshape(b, s, h, p)


def ssd_mixer(z, xbc, dt_raw, conv_w, conv_b, dt_bias, a_log, d_skip, norm_w):
    b, s, _ = z.shape
    f32 = jnp.float32
    xbc = jax.nn.silu(causal_depthwise_conv(xbc, conv_w, conv_b)).astype(f32)
    xs, bm, cm = jnp.split(xbc, [SSD_WIDTH, SSD_WIDTH + SSD_GROUPS * SSD_STATE], axis=-1)
    xs = xs.reshape(b, s, SSD_HEADS, SSD_HEAD_DIM)
    bm = bm.reshape(b, s, SSD_GROUPS, SSD_STATE)
    cm = cm.reshape(b, s, SSD_GROUPS, SSD_STATE)
    dt = jax.nn.softplus(dt_raw.astype(f32) + dt_bias.astype(f32))
    a = -jnp.exp(a_log.astype(f32))
    y = ssd_chunked_scan(xs, dt, a, bm, cm) + d_skip.astype(f32)[:, None] * xs
    y = y.reshape(b, s, SSD_WIDTH) * jax.nn.silu(z.astype(f32))
    yg = y.reshape(b, s, SSD_GROUPS, SSD_WIDTH // SSD_GROUPS)
    yg = yg * lax.rsqrt(jnp.mean(yg * yg, axis=-1, keepdims=True) + SSD_NORM_EPS)
    return yg.reshape(b, s, SSD_WIDTH) * norm_w.astype(f32)


def rglru_mixer(gate_in, x_in, conv_w, conv_b, wa, ba, wx, bx, lam):
    b, s, _ = x_in.shape
    f32 = jnp.float32
    gate = jax.nn.gelu(gate_in.astype(f32))
    xc = causal_depthwise_conv(x_in, conv_w, conv_b).astype(f32)
    xb = xc.reshape(b, s, LRU_BLOCKS, LRU_BLOCK_W)
    r = jax.nn.sigmoid(jnp.einsum('bsnc,ncd->bsnd', xb, wa.astype(f32)).reshape(b, s, LRU_WIDTH) + ba.astype(f32))
    i = jax.nn.sigmoid(jnp.einsum('bsnc,ncd->bsnd', xb, wx.astype(f32)).reshape(b, s, LRU_WIDTH) + bx.astype(f32))
    log_a = -LRU_C * r * jax.nn.softplus(-lam.astype(f32))
    a = jnp.exp(log_a)
    u = jnp.sqrt(-jnp.expm1(2.0 * log_a)) * (i * xc)

    def combine(left, right):
        a_l, h_l = left
        a_r, h_r = right
        return a_l * a_r, a_r * h_l + h_r

    _, h = lax.associative_scan(combine, (a, u), axis=1)
    return h * gate


def swiglu(h, w_gate, w_up, w_down):
    return (jax.nn.silu(h @ w_gate) * (h @ w_up)) @ w_down


def hybrid_layer(x, norm_mix, w_in, ssd_conv_w, ssd_conv_b, ssd_dt_bias, ssd_a_log, ssd_d, ssd_norm,
                 lru_conv_w, lru_conv_b, lru_wa, lru_ba, lru_wx, lru_bx, lru_lambda, w_out,
                 norm_ffn, w_gate, w_up, w_down):
    h = rmsnorm(x, norm_mix)
    q, k, v, z, xbc, dt_raw, g_lru, x_lru = split_cols(h @ w_in)
    att = dilated_attention(q, k, v).astype(x.dtype)
    ssd = ssd_mixer(z, xbc, dt_raw, ssd_conv_w, ssd_conv_b, ssd_dt_bias, ssd_a_log, ssd_d, ssd_norm).astype(x.dtype)
    lru = rglru_mixer(g_lru, x_lru, lru_conv_w, lru_conv_b, lru_wa, lru_ba, lru_wx, lru_bx, lru_lambda).astype(x.dtype)
    x = x + jnp.concatenate([att, ssd, lru], axis=-1) @ w_out
    x = x + swiglu(rmsnorm(x, norm_ffn), w_gate, w_up, w_down)
    return x


def setup_inputs(seed: int = 0) -> dict:
    key = jax.random.key(seed)
    ks = jax.random.split(key, 24)
    f32 = jnp.float32
    L = DEPTH

    def nrm(k, shape, scale):
        return scale * jax.random.normal(k, shape, f32)

    x = nrm(ks[0], (BATCH, SEQ, D_MODEL), 1.0)
    norm_mix = 1.0 + nrm(ks[1], (L, D_MODEL), 0.05)
    w_in = nrm(ks[2], (L, D_MODEL, IN_COLS), D_MODEL ** -0.5)
    ssd_conv_w = nrm(ks[3], (L, CONV_WIDTH, SSD_CONV_CH), CONV_WIDTH ** -0.5)
    ssd_conv_b = nrm(ks[4], (L, SSD_CONV_CH), 0.02)
    dt0 = jnp.exp(jax.random.uniform(ks[5], (L, SSD_HEADS), f32, math.log(DT_MIN), math.log(DT_MAX)))
    ssd_dt_bias = dt0 + jnp.log(-jnp.expm1(-dt0))
    ssd_a_log = jnp.log(jax.random.uniform(ks[6], (L, SSD_HEADS), f32, 1.0, 16.0))
    ssd_d = 1.0 + nrm(ks[7], (L, SSD_HEADS), 0.1)
    ssd_norm = 1.0 + nrm(ks[8], (L, SSD_WIDTH), 0.05)
    lru_conv_w = nrm(ks[9], (L, CONV_WIDTH, LRU_WIDTH), CONV_WIDTH ** -0.5)
    lru_conv_b = nrm(ks[10], (L, LRU_WIDTH), 0.02)
    lru_wa = nrm(ks[11], (L, LRU_BLOCKS, LRU_BLOCK_W, LRU_BLOCK_W), LRU_BLOCK_W ** -0.5)
    lru_ba = nrm(ks[12], (L, LRU_WIDTH), 0.02)
    lru_wx = nrm(ks[13], (L, LRU_BLOCKS, LRU_BLOCK_W, LRU_BLOCK_W), LRU_BLOCK_W ** -0.5)
    lru_bx = nrm(ks[14], (L, LRU_WIDTH), 0.02)
    a_c = jax.random.uniform(ks[15], (L, LRU_WIDTH), f32, 0.9, 0.999)
    a_base = a_c ** (1.0 / LRU_C)
    lru_lambda = jnp.log(a_base) - jnp.log1p(-a_base)
    w_out = nrm(ks[16], (L, D_MIX, D_MODEL), D_MIX ** -0.5)
    norm_ffn = 1.0 + nrm(ks[17], (L, D_MODEL), 0.05)
    w_gate = nrm(ks[18], (L, D_MODEL, D_FF), D_MODEL ** -0.5)
    w_up = nrm(ks[19], (L, D_MODEL, D_FF), D_MODEL ** -0.5)
    w_down = nrm(ks[20], (L, D_FF, D_MODEL), D_FF ** -0.5)
    norm_final = 1.0 + nrm(ks[21], (D_MODEL,), 0.05)
    return {'x': x, 'norm_mix': norm_mix, 'w_in': w_in, 'ssd_conv_w': ssd_conv_w, 'ssd_conv_b': ssd_conv_b,
            'ssd_dt_bias': ssd_dt_bias, 'ssd_a_log': ssd_a_log, 'ssd_d': ssd_d, 'ssd_norm': ssd_norm,
            'lru_conv_w': lru_conv_w, 'lru_conv_b': lru_conv_b, 'lru_wa': lru_wa, 'lru_ba': lru_ba,
            'lru_wx': lru_wx, 'lru_bx': lru_bx, 'lru_lambda': lru_lambda, 'w_out': w_out,
            'norm_ffn': norm_ffn, 'w_gate': w_gate, 'w_up': w_up, 'w_down': w_down, 'norm_final': norm_final}


def reference(x, norm_mix, w_in, ssd_conv_w, ssd_conv_b, ssd_dt_bias, ssd_a_log, ssd_d, ssd_norm,
              lru_conv_w, lru_conv_b, lru_wa, lru_ba, lru_wx, lru_bx, lru_lambda, w_out,
              norm_ffn, w_gate, w_up, w_down, norm_final):
    for l in range(DEPTH):
        x = hybrid_layer(x, norm_mix[l], w_in[l], ssd_conv_w[l], ssd_conv_b[l], ssd_dt_bias[l], ssd_a_log[l],
                         ssd_d[l], ssd_norm[l], lru_conv_w[l], lru_conv_b[l], lru_wa[l], lru_ba[l],
                         lru_wx[l], lru_bx[l], lru_lambda[l], w_out[l], norm_ffn[l], w_gate[l], w_up[l], w_down[l])
    return rmsnorm(x, norm_final)
```

```python
import numpy as np
import ml_dtypes
from contextlib import ExitStack, contextmanager
import concourse.bass as bass
import concourse.mybir as mybir
from concourse.bass_utils import run_bass_kernel_spmd

F32 = mybir.dt.float32
BF16 = mybir.dt.bfloat16
AF = mybir.ActivationFunctionType
ALU = mybir.AluOpType
NPBF = ml_dtypes.bfloat16

T = 2048
NT = 16
D = 1024
KD = 8
DFF = 2816
NFC = 22
NEG = -30000.0
ENGS = ("pe", "act", "dve", "pool", "sp")
NDMASEM = 8
DILS = (1, 4, 16)
SLOPES = [2.0 ** (-8.0 * (h + 1) / 8.0) for h in range(8)]
OFF_Q, OFF_K, OFF_V, OFF_Z, OFF_XBC, OFF_DT, OFF_G, OFF_XL = 0, 512, 1024, 1536, 2048, 3072, 3080, 3592


class Prog:
    def __init__(self, nc, stack, same_engine_sync=True):
        self.nc = nc
        self.same = same_engine_sync
        self.sems = {}
        self.semval = {}
        for e in ENGS:
            self.sems["c_" + e] = stack.enter_context(nc.semaphore("c_" + e))
            self.semval["c_" + e] = 0
        self.dma_rr = {e: 0 for e in ENGS}
        for e in ("sp", "pool", "act"):
            for i in range(NDMASEM):
                nm = "d_%s%d" % (e, i)
                self.sems[nm] = stack.enter_context(nc.semaphore(nm))
                self.semval[nm] = 0
        self.seen = {e: {} for e in ENGS}
        self.lastw = {}
        self.readers = {}
        self.nbuf = 0
        self.eng = {"pe": nc.tensor, "act": nc.scalar, "dve": nc.vector, "pool": nc.gpsimd, "sp": nc.sync}

    def _deps(self, eng, reads, writes):
        need = {}

        def add(tok):
            e2, sn, v = tok
            if e2 == eng and (not self.same or eng == "pe"):
                return
            if need.get(sn, 0) < v:
                need[sn] = v

        for r in reads:
            if r in self.lastw:
                add(self.lastw[r])
        for w in writes:
            if w in self.lastw:
                add(self.lastw[w])
            for t in self.readers.get(w, ()):
                add(t)
        out = []
        for sn, v in need.items():
            if self.seen[eng].get(sn, 0) < v:
                self.seen[eng][sn] = v
                out.append((sn, v))
        return out

    def _emit(self, eng, fn, waits, sn, inc):
        e = self.eng[eng]
        for (wsn, wv) in waits:
            e.wait_ge(self.sems[wsn], wv)
        if fn is not None:
            fn(e).then_inc(self.sems[sn], inc)

    def _commit(self, tok, reads, writes):
        for r in reads:
            self.readers.setdefault(r, []).append(tok)
        for w in writes:
            self.lastw[w] = tok
            self.readers[w] = []

    def op(self, eng, fn, reads=(), writes=()):
        waits = self._deps(eng, reads, writes)
        sn = "c_" + eng
        self.semval[sn] += 1
        tok = (eng, sn, self.semval[sn])
        self._emit(eng, fn, waits, sn, 1)
        self._commit(tok, reads, writes)
        return tok

    def dma(self, fn, reads=(), writes=(), eng="sp"):
        i = self.dma_rr[eng]
        self.dma_rr[eng] = (i + 1) % NDMASEM
        sn = "d_%s%d" % (eng, i)
        waits = self._deps(eng, reads, writes)
        pv = self.semval[sn]
        if pv > 0 and self.seen[eng].get(sn, 0) < pv:
            self.seen[eng][sn] = pv
            waits.append((sn, pv))
        self.semval[sn] += 16
        tok = ("dma_" + eng, sn, self.semval[sn])
        self._emit(eng, fn, waits, sn, 16)
        self._commit(tok, reads, writes)
        return tok

    def barrier(self, engs=ENGS):
        for eng in engs:
            waits = []
            for sn, v in self.semval.items():
                if v > 0 and self.seen[eng].get(sn, 0) < v:
                    if sn == "c_" + eng:
                        continue
                    self.seen[eng][sn] = v
                    waits.append((sn, v))
            self._emit(eng, None, waits, None, 0)

    @contextmanager
    def scope(self):
        with ExitStack() as st:
            yield Scope(self, st)
            self.barrier()
            self.lastw.clear()
            self.readers.clear()


class Scope:
    def __init__(self, P, st):
        self.P = P
        self.st = st

    def sb(self, shape, dt, name=None):
        self.P.nbuf += 1
        return self.st.enter_context(self.P.nc.sbuf_tensor("%s_%d" % (name or "sb", self.P.nbuf), list(shape), dt))

    def ps(self, shape=(128, 512), dt=F32, name=None):
        self.P.nbuf += 1
        return self.st.enter_context(self.P.nc.psum_tensor("%s_%d" % (name or "ps", self.P.nbuf), list(shape), dt))


def sl(start, step, n=128):
    return slice(start, start + (n - 1) * step + 1, step)


def new_nc():
    return bass.Bass("TRN2", target_bir_lowering=False)


def dram_in(nc, name, shape, dt):
    return nc.dram_tensor(name, list(shape), dt, kind="ExternalInput").ap()


def dram_out(nc, name, shape, dt):
    return nc.dram_tensor(name, list(shape), dt, kind="ExternalOutput").ap()


def load_x(P, x_dram, x_sb):
    xv = x_dram.rearrange("(n p) d -> p n d", p=128)
    for i in range(4):
        P.dma(lambda e: e.dma_start(out=x_sb[:, 4 * i:4 * i + 4, :], in_=xv[:, 4 * i:4 * i + 4, :]),
              writes=["x%d" % t for t in range(4 * i, 4 * i + 4)])


def stage_norm_T(P, S, x_sb, hT, ident_bf, eps=1e-6):
    ssq = S.sb([128, NT], F32, "ssq")
    rstd = S.sb([128, NT], F32, "rstd")
    junk = S.sb([128, D], BF16, "junk")
    xn = [S.sb([128, D], BF16, "xn") for _ in range(2)]
    pst = [S.ps([128, 8, 128], BF16, "pst") for _ in range(2)]
    P.op("dve", lambda e: e.memset(ssq[:], 0.0), writes=["ssq"])
    for t in range(NT):
        P.op("act", lambda e: e.activation(out=junk[:], in_=x_sb[:, t, :], func=AF.Square, accum_out=ssq[:, t:t + 1]),
             reads=["x%d" % t, "ssq"], writes=["junk", "ssq%d" % t])
    P.op("act", lambda e: e.activation(out=rstd[:], in_=ssq[:], func=AF.Sqrt, bias=eps, scale=1.0 / D),
         reads=["ssq%d" % t for t in range(NT)], writes=["std"])
    P.op("dve", lambda e: e.reciprocal(out=rstd[:], in_=rstd[:]), reads=["std"], writes=["rstd"])
    for t in range(NT):
        b = t % 2
        P.op("dve", lambda e: e.tensor_scalar(out=xn[b][:], in0=x_sb[:, t, :], scalar1=rstd[:, t:t + 1], scalar2=None, op0=ALU.mult),
             reads=["x%d" % t, "rstd"], writes=["xn%d" % b])
        for k in range(KD):
            P.op("pe", lambda e: e.transpose(out=pst[b][:, k, :], in_=xn[b][:, k * 128:(k + 1) * 128], identity=ident_bf[:]),
                 reads=["xn%d" % b, "ident"], writes=["pst%d" % b])
        P.op("act", lambda e: e.copy(out=hT[:, :, t * 128:(t + 1) * 128], in_=pst[b][:]),
             reads=["pst%d" % b], writes=["hT%d" % t])


def make_ident(P, S, name="ident"):
    ident = S.sb([128, 128], BF16, "ident")
    P.op("pool", lambda e: e.memset(ident[:], 1.0), writes=[name])
    P.op("pool", lambda e: e.affine_select(out=ident[:], in_=ident[:], pattern=[[-1, 128]], compare_op=ALU.is_equal,
                                            fill=0.0, base=0, channel_multiplier=1), reads=[name], writes=[name])
    return ident


class WStream:
    def __init__(self, P, S, kc, ncols, nbuf=2, tag="w", nstg=None):
        self.P, self.kc, self.ncols, self.nbuf, self.tag = P, kc, ncols, nbuf, tag
        self.nstg = nstg or nbuf
        self.stg = [S.sb([128, kc, ncols], F32, tag + "stg") for _ in range(self.nstg)]
        self.wbf = [S.sb([128, kc, ncols], BF16, tag + "bf") for _ in range(nbuf)]
        self.i = 0

    def load(self, w_dram, row0, col0, kc=None, ncols=None, gain=None, gain_res=None, k0=0):
        P = self.P
        kc = kc or self.kc
        ncols = ncols or self.ncols
        b = self.i % self.nbuf
        self.i += 1
        bs = (self.i - 1) % self.nstg
        sn, wn = "%sstg%d" % (self.tag, bs), "%sbf%d" % (self.tag, b)
        stg, wbf = self.stg[bs], self.wbf[b]
        src = w_dram[row0:row0 + kc * 128, col0:col0 + ncols].rearrange("(k p) n -> p k n", p=128)
        half = max(1, kc // 2)
        for (a, bnd) in ((0, half), (half, kc)):
            if bnd > a:
                P.dma(lambda e: e.dma_start(out=stg[:, a:bnd, :ncols], in_=src[:, a:bnd, :]), writes=[sn + "_%d" % a])
        for k in range(kc):
            part = sn + "_%d" % (0 if k < half else half)
            eng = "pool" if k % 2 == 0 else "dve"
            if gain is not None:
                P.op(eng, lambda e: e.tensor_scalar(out=wbf[:, k, :ncols], in0=stg[:, k, :ncols], scalar1=gain[:, k0 + k:k0 + k + 1],
                                                    scalar2=None, op0=ALU.mult),
                     reads=[part, gain_res], writes=[wn + "_%d" % k])
            else:
                P.op(eng, lambda e: e.tensor_copy(out=wbf[:, k, :ncols], in_=stg[:, k, :ncols]), reads=[part], writes=[wn + "_%d" % k])
        return wbf, [wn + "_%d" % k for k in range(kc)]


def build_LA(upto=99):
    nc = new_nc()
    x = dram_in(nc, "x", [T, D], F32)
    w_in = dram_in(nc, "w_in", [D, 4104], F32)
    gmix = dram_in(nc, "gmix", [128, KD], F32)
    qT = dram_out(nc, "qT", [128, 4, T], BF16)
    kT = dram_out(nc, "kT", [128, 4, T], BF16)
    vU = dram_out(nc, "vU", [3, 16, 128, 512], BF16)
    zo = dram_out(nc, "z", [T, 512], F32)
    dto = dram_out(nc, "dtr", [T, 64], F32)
    xbcT = dram_out(nc, "xbcT", [128, 8, T], F32)
    gT = dram_out(nc, "gT", [128, 4, T], F32)
    xlT = dram_out(nc, "xlT", [128, 4, T], F32)
    with ExitStack() as st:
        P = Prog(nc, st)
        with P.scope() as S0:
            hT = S0.sb([128, KD, T], BF16, "hT")
            g_sb = S0.sb([128, KD], F32, "g")
            P.dma(lambda e: e.dma_start(out=g_sb[:], in_=gmix), writes=["g"])
            with P.scope() as S:
                x_sb = S.sb([128, NT, D], F32, "x")
                ident = make_ident(P, S)
                load_x(P, x, x_sb)
                stage_norm_T(P, S, x_sb, hT, ident)
            hres = ["hT%d" % t for t in range(NT)]
            with P.scope() as S:
                ws = WStream(P, S, KD, 512, 2, "win")
                psb = [S.ps([128, 512], F32, "pp") for _ in range(4)]
                rowf = [S.sb([128, T], F32, "rowf") for _ in range(2)]
                rowb = [S.sb([128, T], BF16, "rowb") for _ in range(2)]
                tmf = [S.sb([128, 4, 512], F32, "tmf") for _ in range(2)]
                tmb = [S.sb([128, 4, 512], BF16, "tmb") for _ in range(2)]
                cnt = {"ps": 0, "row": 0, "tm": 0}

                def fm_group(col0, dest, chunk0, dt_is_bf, scale=None):
                    wbf, wres = ws.load(w_in, 0, col0, gain=g_sb, gain_res="g")
                    for c in range(4):
                        rb = cnt["row"] % 2
                        cnt["row"] += 1
                        row = (rowb if dt_is_bf else rowf)[rb]
                        rname = ("rowb%d" if dt_is_bf else "rowf%d") % rb
                        for tb in range(4):
                            pb = cnt["ps"] % 4
                            cnt["ps"] += 1
                            for k in range(KD):
                                P.op("pe", lambda e: e.matmul(psb[pb][:], lhsT=wbf[:, k, c * 128:(c + 1) * 128], rhs=hT[:, k, tb * 512:(tb + 1) * 512],
                                                               start=(k == 0), stop=(k == KD - 1)),
                                     reads=[wres[k]] + hres[4 * tb:4 * tb + 4], writes=["pp%d" % pb])
                            if scale is not None:
                                P.op("act", lambda e: e.mul(out=row[:, tb * 512:(tb + 1) * 512], in_=psb[pb][:], mul=scale),
                                     reads=["pp%d" % pb], writes=[rname + "_%d" % tb])
                            elif tb % 2 == 0:
                                P.op("act", lambda e: e.copy(out=row[:, tb * 512:(tb + 1) * 512], in_=psb[pb][:]),
                                     reads=["pp%d" % pb], writes=[rname + "_%d" % tb])
                            else:
                                P.op("dve", lambda e: e.tensor_copy(out=row[:, tb * 512:(tb + 1) * 512], in_=psb[pb][:]),
                                     reads=["pp%d" % pb], writes=[rname + "_%d" % tb])
                        P.dma(lambda e: e.dma_start(out=dest[:, chunk0 + c, :], in_=row[:]),
                              reads=[rname + "_%d" % tb for tb in range(4)], writes=["out"])

                def tm_group(col0, units, dest_fn, dt_is_bf, ncols=512, ocols=(0, 512)):
                    wbf, wres = ws.load(w_in, 0, col0, ncols=ncols, gain=g_sb, gain_res="g")
                    for i0 in range(0, len(units), 4):
                        ob = cnt["tm"] % 2
                        cnt["tm"] += 1
                        ot = (tmb if dt_is_bf else tmf)[ob]
                        oname = ("tmb%d" if dt_is_bf else "tmf%d") % ob
                        for j in range(4):
                            start, step = units[i0 + j]
                            pb = cnt["ps"] % 4
                            cnt["ps"] += 1
                            for k in range(KD):
                                P.op("pe", lambda e: e.matmul(psb[pb][:, :ncols], lhsT=hT[:, k, sl(start, step)], rhs=wbf[:, k, :ncols],
                                                               start=(k == 0), stop=(k == KD - 1)),
                                     reads=[wres[k]] + hres, writes=["pp%d" % pb])
                            if j % 2 == 0:
                                P.op("act", lambda e: e.copy(out=ot[:, j, :ncols], in_=psb[pb][:, :ncols]), reads=["pp%d" % pb], writes=[oname + "_%d" % j])
                            else:
                                P.op("dve", lambda e: e.tensor_copy(out=ot[:, j, :ncols], in_=psb[pb][:, :ncols]), reads=["pp%d" % pb], writes=[oname + "_%d" % j])
                        P.dma(lambda e: e.dma_start(out=dest_fn(i0), in_=ot[:, :, ocols[0]:ocols[1]]), reads=[oname + "_%d" % j for j in range(4)], writes=["out"])

                fm_group(OFF_Q, qT, 0, True, scale=0.125)
                if upto >= 2:
                    fm_group(OFF_K, kT, 0, True)
                for di, d in enumerate(DILS if upto >= 3 else ()):
                    if d == 1:
                        units = [(n * 128, 1) for n in range(16)]
                    elif d == 4:
                        units = [(r + 512 * n, 4) for r in range(4) for n in range(4)]
                    else:
                        units = [(r, 16) for r in range(16)]
                    tm_group(OFF_V, units, lambda i0, di=di: vU[di, i0:i0 + 4].rearrange("u p n -> p u n"), True)
                plain = [(n * 128, 1) for n in range(16)]
                if upto < 4:
                    P.barrier()
                    return nc
                tm_group(OFF_Z, plain, lambda i0: zo[i0 * 128:(i0 + 4) * 128, :].rearrange("(u p) n -> p u n", p=128), False)
                tm_group(OFF_DT + 8 - 512, plain, lambda i0: dto[i0 * 128:(i0 + 4) * 128, :].rearrange("(u p) n -> p u n", p=128), False, ocols=(448, 512))
                fm_group(OFF_XBC, xbcT, 0, False)
                fm_group(OFF_XBC + 512, xbcT, 4, False)
                fm_group(OFF_G, gT, 0, False)
                fm_group(OFF_XL, xlT, 0, False)
                P.barrier()
    return nc


def att_units(d):
    if d == 1:
        return [(n * 128, n == 0) for n in range(16)]
    if d == 4:
        return [(r + 512 * n, n == 0) for r in range(4) for n in range(4)]
    return [(r, True) for r in range(16)]


def stage_attention(P, S, qT_d, kTe_d, vE_d, base_d, attT):
    base = S.sb([128, 3, 4, 128], F32, "base")
    P.dma(lambda e: e.dma_start(out=base[:], in_=base_d.rearrange("d p v i -> p d v i")), writes=["base"])
    ones = S.sb([128, 128], BF16, "ones")
    P.op("pool", lambda e: e.memset(ones[:], 1.0), writes=["ones"])
    nu = {0: 17, 1: 20, 2: 32}
    qp = [S.sb([128, T], BF16, "qp") for _ in range(2)]
    kp = [S.sb([128, 2 * T], BF16, "kp") for _ in range(2)]
    vp = [[S.sb([128, nu[di], 128], BF16, "vp%d" % di) for di in range(3)] for _ in range(2)]
    acc = S.sb([128, 2, T], F32, "acc")
    rec = S.sb([128, T], F32, "rec")
    s_ps = [S.ps([128, 2, 128], F32, "sps") for _ in range(2)]
    o_ps = [S.ps([128, 2, 128], F32, "ops") for _ in range(2)]
    tmp = [S.sb([128, 2, 128], F32, "tmp") for _ in range(2)]
    pt = [S.sb([128, 2, 128], BF16, "pt") for _ in range(2)]
    it = 0
    for pr in range(4):
        pb = pr % 2
        P.dma(lambda e: e.dma_start(out=qp[pb][:], in_=qT_d[:, pr, :]), writes=["qp%d" % pb])
        P.dma(lambda e: e.dma_start(out=kp[pb][:], in_=kTe_d[:, pr, :]), writes=["kp%d" % pb])
        for di in range(3):
            P.dma(lambda e: e.dma_start(out=vp[pb][di][:], in_=vE_d[di][pr]),
                  writes=["vp%d_%d" % (pb, di)], eng="pool")
        first = True
        for di in (2, 1, 0):
            d = DILS[di]
            for ui, (start, is_first) in enumerate(att_units(d)):
                if d == 1:
                    uc = ui + 1
                    up = ui
                elif d == 4:
                    r, n = ui // 4, ui % 4
                    uc = r * 5 + n + 1
                    up = r * 5 + n
                else:
                    uc = ui * 2 + 1
                    up = ui * 2
                var = 2 if is_first else 0
                for e2 in range(2):
                    h = 2 * pr + e2
                    sb_ = it % 2
                    it += 1
                    prow = slice(e2 * 64, (e2 + 1) * 64)
                    qs = sl(start, d)
                    kc = sl(T + start, d)
                    kpv = sl(T + start - 128 * d, d)
                    P.op("pe", lambda e: e.matmul(s_ps[sb_][:, 0, :], lhsT=kp[pb][prow, kpv], rhs=qp[pb][prow, qs], start=True, stop=True),
                         reads=["kp%d" % pb, "qp%d" % pb], writes=["sps%d" % sb_])
                    P.op("pe", lambda e: e.matmul(s_ps[sb_][:, 1, :], lhsT=kp[pb][prow, kc], rhs=qp[pb][prow, qs], start=True, stop=True),
                         reads=["kp%d" % pb, "qp%d" % pb], writes=["sps%d" % sb_])
                    P.op("dve", lambda e: e.scalar_tensor_tensor(out=tmp[sb_][:], in0=base[:, di, var:var + 2, :], scalar=SLOPES[h], in1=s_ps[sb_][:],
                                                                 op0=ALU.mult, op1=ALU.add),
                         reads=["base", "sps%d" % sb_], writes=["tmp%d" % sb_])
                    P.op("act", lambda e: e.activation(out=pt[sb_][:], in_=tmp[sb_][:], func=AF.Exp), reads=["tmp%d" % sb_], writes=["pt%d" % sb_])
                    vprev = vp[pb][di][:, up, :]
                    vcur = vp[pb][di][:, uc, :]
                    vres = ["vp%d_%d" % (pb, di)]
                    P.op("pe", lambda e: e.matmul(o_ps[sb_][:, 0, :], lhsT=vprev, rhs=pt[sb_][:, 0, :], start=True, stop=False),
                         reads=vres + ["pt%d" % sb_], writes=["ops%d" % sb_])
                    P.op("pe", lambda e: e.matmul(o_ps[sb_][:, 0, :], lhsT=vcur, rhs=pt[sb_][:, 1, :], start=False, stop=True),
                         reads=vres + ["pt%d" % sb_], writes=["ops%d" % sb_])
                    P.op("pe", lambda e: e.matmul(o_ps[sb_][:, 1, :], lhsT=ones[:], rhs=pt[sb_][:, 0, :], start=True, stop=False),
                         reads=["ones", "pt%d" % sb_], writes=["ops%d" % sb_])
                    P.op("pe", lambda e: e.matmul(o_ps[sb_][:, 1, :], lhsT=ones[:], rhs=pt[sb_][:, 1, :], start=False, stop=True),
                         reads=["ones", "pt%d" % sb_], writes=["ops%d" % sb_])
                    if first:
                        P.op("dve", lambda e: e.tensor_copy(out=acc[prow, :, qs], in_=o_ps[sb_][prow, :, :]),
                             reads=["ops%d" % sb_], writes=["acc%d" % e2])
                    else:
                        P.op("dve", lambda e: e.tensor_tensor(out=acc[prow, :, qs], in0=acc[prow, :, qs], in1=o_ps[sb_][prow, :, :], op=ALU.add),
                             reads=["ops%d" % sb_, "acc%d" % e2], writes=["acc%d" % e2])
            first = False
        P.op("dve", lambda e: e.reciprocal(out=rec[:], in_=acc[:, 1, :]), reads=["acc0", "acc1"], writes=["rec"])
        P.op("dve", lambda e: e.tensor_tensor(out=attT[:, pr, :], in0=acc[:, 0, :], in1=rec[:], op=ALU.mult),
             reads=["acc0", "acc1", "rec"], writes=["attT%d" % pr])


def conv_fm(P, S, bufs, ext_d, cc, cw, cb, cwres, i):
    b = i % 2
    ext, acc = bufs["ext"][b], bufs["acc"][b]
    P.dma(lambda e: e.dma_start(out=ext[:], in_=ext_d[:, cc, :]), writes=["ext%d" % b])
    eng = "dve"
    P.op(eng, lambda e: e.tensor_scalar(out=acc[:], in0=ext[:, 0:T], scalar1=cw[:, cc, 0:1], scalar2=cb[:, cc:cc + 1], op0=ALU.mult, op1=ALU.add),
         reads=["ext%d" % b, cwres], writes=["cacc%d" % b])
    for k in (1, 2, 3):
        P.op(eng, lambda e: e.scalar_tensor_tensor(out=acc[:], in0=ext[:, k:k + T], scalar=cw[:, cc, k:k + 1], in1=acc[:], op0=ALU.mult, op1=ALU.add),
             reads=["ext%d" % b, cwres, "cacc%d" % b], writes=["cacc%d" % b])
    return acc, "cacc%d" % b


def stage_ssd(P, S, full, xbce_d, dtr_d, z_d, pp, ppres, cf, cfres, ident_bf, Sprev_d, lprev_d, ssdT, send_d, ltot_d, dbg=None):
    xs = S.sb([128, NT, 512], BF16, "xs")
    Btm = S.sb([128, NT, 256], BF16, "Btm")
    BT = S.sb([128, 2, T], BF16, "BT")
    CT = S.sb([128, 2, T], BF16, "CT")
    dtr = S.sb([128, NT, 64], F32, "dtr")
    dt = S.sb([128, NT, 8], F32, "dt")
    dtA = S.sb([128, NT, 8], F32, "dtA")
    acs = S.sb([128, NT, 8], F32, "acs")
    tot = S.sb([128, NT, 8], F32, "tot")
    wv = S.sb([128, NT, 8], F32, "wv")
    dec = S.sb([128, NT, 8], F32, "dec")
    eacs = S.sb([128, NT, 8], F32, "eacs")
    aneg = S.sb([128, 8], F32, "aneg")
    St = S.sb([128, 512], F32, "St")
    Sbf = S.sb([128, 512], BF16, "Sbf")
    P.dma(lambda e: e.dma_start(out=dtr[:], in_=dtr_d.rearrange("(c p) n -> p c n", p=128)), writes=["dtr"])
    P.op("act", lambda e: e.activation(out=aneg[:], in_=pp["alog"], func=AF.Exp), reads=[ppres], writes=["aneg"])
    P.op("dve", lambda e: e.tensor_tensor(out=dt[:], in0=dtr[:, :, 56:64], in1=pp["dtb"].unsqueeze(1).to_broadcast([128, NT, 8]), op=ALU.add),
         reads=["dtr", ppres], writes=["dt"])
    P.op("act", lambda e: e.activation(out=dt[:], in_=dt[:], func=AF.Exp), reads=["dt"], writes=["dt"])
    P.op("act", lambda e: e.activation(out=dt[:], in_=dt[:], func=AF.Ln, bias=1.0), reads=["dt"], writes=["dt"])
    P.op("dve", lambda e: e.scalar_tensor_tensor(out=dtA[:], in0=dt[:], scalar=-1.0, in1=aneg[:].unsqueeze(1).to_broadcast([128, NT, 8]),
                                                 op0=ALU.mult, op1=ALU.mult), reads=["dt", "aneg"], writes=["dtA"])
    with P.scope() as S1:
        ps_a = S1.ps([128, 128], F32, "psa")
        ps_t = S1.ps([128, 128], F32, "pst")
        dtAf = dtA[:].rearrange("p c h -> p (c h)")
        P.op("pe", lambda e: e.matmul(ps_a[:], lhsT=cf["U"], rhs=dtAf, start=True, stop=True), reads=["dtA", cfres], writes=["psa"])
        P.op("pe", lambda e: e.matmul(ps_t[:], lhsT=cf["onesf"], rhs=dtAf, start=True, stop=True), reads=["dtA", cfres], writes=["pst"])
        P.op("dve", lambda e: e.tensor_copy(out=acs[:].rearrange("p c h -> p (c h)"), in_=ps_a[:]), reads=["psa"], writes=["acs"])
        P.op("dve", lambda e: e.tensor_copy(out=tot[:].rearrange("p c h -> p (c h)"), in_=ps_t[:]), reads=["pst"], writes=["tot"])
        P.op("dve", lambda e: e.tensor_tensor(out=wv[:], in0=tot[:], in1=acs[:], op=ALU.subtract), reads=["tot", "acs"], writes=["wv"])
        P.op("act", lambda e: e.activation(out=wv[:], in_=wv[:], func=AF.Exp), reads=["wv"], writes=["wv"])
        P.op("dve", lambda e: e.tensor_tensor(out=wv[:], in0=wv[:], in1=dt[:], op=ALU.mult), reads=["wv", "dt"], writes=["wv"])
        P.op("act", lambda e: e.activation(out=dec[:], in_=tot[:], func=AF.Exp), reads=["tot"], writes=["dec"])
        P.op("act", lambda e: e.activation(out=eacs[:], in_=acs[:], func=AF.Exp), reads=["acs"], writes=["eacs"])
        bufs = {"ext": [S1.sb([128, T + 3], F32, "ext") for _ in range(2)], "acc": [S1.sb([128, T], F32, "cacc") for _ in range(2)]}
        fmt = [S1.sb([128, T], BF16, "fmt") for _ in range(2)]
        ptr = [S1.ps([128, 8, 128], BF16, "ptr") for _ in range(2)]
        nt = 0
        for cc in range(8):
            acc_t, ares = conv_fm(P, S1, bufs, xbce_d, cc, pp["scw"], pp["scb"], ppres, cc)
            if cc < 6:
                dst = fmt[cc % 2] if cc < 4 else None
                dres = "fmt%d" % (cc % 2) if cc < 4 else "BT%d" % (cc - 4)
                out_ap = dst[:] if cc < 4 else BT[:, cc - 4, :]
            else:
                dres = "CT%d" % (cc - 6)
                out_ap = CT[:, cc - 6, :]
            P.op("act", lambda e: e.activation(out=out_ap, in_=acc_t[:], func=AF.Silu), reads=[ares], writes=[dres])
            if cc < 6:
                for t8 in range(2):
                    pb = nt % 2
                    nt += 1
                    for j in range(8):
                        t = t8 * 8 + j
                        P.op("pe", lambda e: e.transpose(out=ptr[pb][:, j, :], in_=out_ap[:, t * 128:(t + 1) * 128], identity=ident_bf[:]),
                             reads=[dres, "ident"], writes=["ptr%d" % pb])
                    if cc < 4:
                        P.op("dve", lambda e: e.tensor_copy(out=xs[:, t8 * 8:(t8 + 1) * 8, cc * 128:(cc + 1) * 128], in_=ptr[pb][:]),
                             reads=["ptr%d" % pb], writes=["xs_%d_%d" % (cc, t8)])
                    else:
                        P.op("dve", lambda e: e.tensor_copy(out=Btm[:, t8 * 8:(t8 + 1) * 8, (cc - 4) * 128:(cc - 3) * 128], in_=ptr[pb][:]),
                             reads=["ptr%d" % pb], writes=["Btm_%d_%d" % (cc - 4, t8)])
    xsres = ["xs_%d_%d" % (cc, t8) for cc in range(4) for t8 in range(2)]
    btres = ["Btm_%d_%d" % (g, t8) for g in range(2) for t8 in range(2)]
    P.op("dve", lambda e: e.memset(St[:], 0.0), writes=["St"])
    if full:
        with P.scope() as S1:
            sp = S1.sb([128, 3, 512], F32, "sp")
            lp = S1.sb([128, 3, 8], F32, "lp")
            P.dma(lambda e: e.dma_start(out=sp[:], in_=Sprev_d.rearrange("s p n -> p s n")), writes=["sp"])
            P.dma(lambda e: e.dma_start(out=lp[:], in_=lprev_d.rearrange("s p n -> p s n")), writes=["lp"])
            P.op("act", lambda e: e.activation(out=lp[:], in_=lp[:], func=AF.Exp), reads=["lp"], writes=["lp"])
            for s_ in range(3):
                P.op("dve", lambda e: e.tensor_tensor(out=St[:].rearrange("p (h d) -> p h d", h=8), in0=St[:].rearrange("p (h d) -> p h d", h=8),
                                                      in1=lp[:, s_, :].unsqueeze(2).to_broadcast([128, 8, 64]), op=ALU.mult),
                     reads=["St", "lp"], writes=["St"])
                P.op("dve", lambda e: e.tensor_tensor(out=St[:], in0=St[:], in1=sp[:, s_, :], op=ALU.add), reads=["St", "sp"], writes=["St"])
    P.op("act", lambda e: e.copy(out=Sbf[:], in_=St[:]), reads=["St"], writes=["Sbf"])
    with P.scope() as S1:
        xw = [S1.sb([128, 512], BF16, "xw") for _ in range(2)]
        ps_s = [S1.ps([128, 512], F32, "pss") for _ in range(1)]
        if full:
            dtAm = [S1.sb([128, 8, 128], F32, "dtAm") for _ in range(2)]
            Lt = [S1.sb([128, 4, 128], F32, "Lt") for _ in range(2)]
            MT = [S1.sb([128, 4, 128], BF16, "MT") for _ in range(2)]
            GTm = [S1.sb([128, 128], F32, "GTm") for _ in range(2)]
            xdt = [S1.sb([128, 512], BF16, "xdt") for _ in range(2)]
            yt = [S1.sb([128, 512], F32, "yt") for _ in range(2)]
            zt = [S1.sb([128, 512], F32, "zt") for _ in range(2)]
            ybf = [S1.sb([128, 512], BF16, "ybf") for _ in range(2)]
            sq = S1.sb([128, 256], BF16, "sqj")
            gss = [S1.sb([128, 2], F32, "gss") for _ in range(2)]
            ps_seg = [S1.ps([128, 4, 128], F32, "pseg") for _ in range(2)]
            ps_g = S1.ps([128, 2, 128], F32, "psg")
            ps_yd = S1.ps([128, 512], F32, "psyd")
            ps_yo = S1.ps([128, 512], F32, "psyo")
            ps_tr = S1.ps([128, 4, 128], BF16, "pstr")
        h3 = lambda ap: ap.rearrange("p (h d) -> p h d", h=8)
        for c in range(NT):
            b = c % 2
            cs = slice(c * 128, (c + 1) * 128)
            if full:
                P.dma(lambda e: e.dma_start(out=zt[b][:], in_=z_d[c * 128:(c + 1) * 128, :]), writes=["zt%d" % b])
                P.op("pool", lambda e: e.tensor_tensor(out=dtAm[b][:], in0=cf["lstrict3"], in1=dtA[:, c, :].unsqueeze(2).to_broadcast([128, 8, 128]), op=ALU.mult),
                     reads=["dtA", cfres], writes=["dtAm%d" % b])
                P.op("pool", lambda e: e.tensor_tensor(out=h3(xdt[b][:]), in0=h3(xs[:, c, :]), in1=dt[:, c, :].unsqueeze(2).to_broadcast([128, 8, 64]), op=ALU.mult),
                     reads=xsres + ["dt"], writes=["xdt%d" % b])
                for g in range(2):
                    P.op("pe", lambda e: e.matmul(ps_g[:, g, :], lhsT=BT[:, g, cs], rhs=CT[:, g, cs], start=True, stop=True),
                         reads=["BT%d" % g, "CT%d" % g], writes=["psg%d" % g])
                    for h in range(4):
                        P.op("pe", lambda e: e.matmul(ps_seg[g][:, h, :], lhsT=dtAm[b][:, 4 * g + h, :], rhs=cf["U"], start=True, stop=True),
                             reads=["dtAm%d" % b, cfres], writes=["pseg%d" % g])
                    P.op("act", lambda e: e.activation(out=Lt[g][:], in_=ps_seg[g][:], func=AF.Exp), reads=["pseg%d" % g], writes=["Lt%d" % g])
                    P.op("dve", lambda e: e.tensor_tensor(out=GTm[g][:], in0=ps_g[:, g, :], in1=cf["U"], op=ALU.mult), reads=["psg%d" % g, cfres], writes=["GTm%d" % g])
                    P.op("dve", lambda e: e.tensor_tensor(out=MT[g][:], in0=Lt[g][:], in1=GTm[g][:].unsqueeze(1).to_broadcast([128, 4, 128]), op=ALU.mult),
                         reads=["Lt%d" % g, "GTm%d" % g], writes=["MT%d" % g])
                    for h in range(4):
                        hh = 4 * g + h
                        P.op("pe", lambda e: e.matmul(ps_yd[:, hh * 64:(hh + 1) * 64], lhsT=MT[g][:, h, :], rhs=xdt[b][:, hh * 64:(hh + 1) * 64], start=True, stop=True),
                             reads=["MT%d" % g, "xdt%d" % b], writes=["psyd"])
                    P.op("pe", lambda e: e.matmul(ps_yo[:, g * 256:(g + 1) * 256], lhsT=CT[:, g, cs], rhs=Sbf[:, g * 256:(g + 1) * 256], start=True, stop=True),
                         reads=["CT%d" % g, "Sbf"], writes=["psyo"])
            P.op("dve", lambda e: e.tensor_tensor(out=h3(xw[b][:]), in0=h3(xs[:, c, :]), in1=wv[:, c, :].unsqueeze(2).to_broadcast([128, 8, 64]), op=ALU.mult),
                 reads=xsres + ["wv"], writes=["xw%d" % b])
            for g in range(2):
                P.op("pe", lambda e: e.matmul(ps_s[0][:, g * 256:(g + 1) * 256], lhsT=Btm[:, c, g * 128:(g + 1) * 128], rhs=xw[b][:, g * 256:(g + 1) * 256], start=True, stop=True),
                     reads=btres + ["xw%d" % b], writes=["pss"])
            P.op("dve", lambda e: e.tensor_tensor(out=h3(St[:]), in0=h3(St[:]), in1=dec[:, c, :].unsqueeze(2).to_broadcast([128, 8, 64]), op=ALU.mult),
                 reads=["St", "dec"], writes=["St"])
            P.op("dve", lambda e: e.tensor_tensor(out=St[:], in0=St[:], in1=ps_s[0][:], op=ALU.add), reads=["St", "pss"], writes=["St"])
            if full:
                P.op("dve", lambda e: e.tensor_tensor(out=h3(yt[b][:]), in0=h3(ps_yo[:]), in1=eacs[:, c, :].unsqueeze(2).to_broadcast([128, 8, 64]), op=ALU.mult),
                     reads=["psyo", "eacs"], writes=["yt%d" % b])
                P.op("dve", lambda e: e.tensor_tensor(out=yt[b][:], in0=yt[b][:], in1=ps_yd[:], op=ALU.add), reads=["yt%d" % b, "psyd"], writes=["yt%d" % b])
                if dbg is not None:
                    P.dma(lambda e: e.dma_start(out=dbg["y0"][c * 128:(c + 1) * 128, :], in_=yt[b][:]), reads=["yt%d" % b], writes=["o_dbg"])
                    if c == 0:
                        P.dma(lambda e: e.dma_start(out=dbg["L"], in_=Lt[0][:].rearrange("p h i -> p (h i)")), reads=["Lt0"], writes=["o_dbg2"])
                        P.dma(lambda e: e.dma_start(out=dbg["xs"], in_=xs[:, 0, :]), reads=xsres, writes=["o_dbg3"])
                P.op("pool", lambda e: e.tensor_tensor(out=xdt[b][:], in0=xs[:, c, :], in1=pp["dsk"], op=ALU.mult), reads=xsres + [ppres, "xdt%d" % b], writes=["xdt%d" % b])
                P.op("dve", lambda e: e.tensor_tensor(out=yt[b][:], in0=yt[b][:], in1=xdt[b][:], op=ALU.add), reads=["yt%d" % b, "xdt%d" % b], writes=["yt%d" % b])
                P.op("act", lambda e: e.activation(out=zt[b][:], in_=zt[b][:], func=AF.Silu), reads=["zt%d" % b], writes=["zt%d" % b])
                P.op("dve", lambda e: e.tensor_tensor(out=yt[b][:], in0=yt[b][:], in1=zt[b][:], op=ALU.mult), reads=["yt%d" % b, "zt%d" % b], writes=["yt%d" % b])
                P.op("dve", lambda e: e.memset(gss[b][:], 0.0), writes=["gss%d" % b])
                for g in range(2):
                    P.op("act", lambda e: e.activation(out=sq[:], in_=yt[b][:, g * 256:(g + 1) * 256], func=AF.Square, accum_out=gss[b][:, g:g + 1]),
                         reads=["yt%d" % b, "gss%d" % b], writes=["sqj", "gss%d_%d" % (b, g)])
                P.op("act", lambda e: e.activation(out=gss[b][:], in_=gss[b][:], func=AF.Sqrt, bias=1e-5, scale=1.0 / 256),
                     reads=["gss%d_0" % b, "gss%d_1" % b], writes=["gss%d" % b])
                P.op("dve", lambda e: e.reciprocal(out=gss[b][:], in_=gss[b][:]), reads=["gss%d" % b], writes=["gss%d" % b])
                for g in range(2):
                    gsl = slice(g * 256, (g + 1) * 256)
                    P.op("dve", lambda e: e.scalar_tensor_tensor(out=ybf[b][:, gsl], in0=yt[b][:, gsl], scalar=gss[b][:, g:g + 1], in1=pp["snw"][:, gsl],
                                                                 op0=ALU.mult, op1=ALU.mult), reads=["yt%d" % b, "gss%d" % b, ppres], writes=["ybf%d_%d" % (b, g)])
                for k in range(4):
                    P.op("pe", lambda e: e.transpose(out=ps_tr[:, k, :], in_=ybf[b][:, k * 128:(k + 1) * 128], identity=ident_bf[:]),
                         reads=["ybf%d_%d" % (b, k // 2), "ident"], writes=["pstr"])
                P.op("act", lambda e: e.copy(out=ssdT[:, :, cs], in_=ps_tr[:]), reads=["pstr"], writes=["ssdT%d" % c])
            P.op("act", lambda e: e.copy(out=Sbf[:], in_=St[:]), reads=["St"], writes=["Sbf"])
        if not full:
            lt = S1.sb([128, 8], F32, "lt")
            P.op("dve", lambda e: e.tensor_reduce(out=lt[:], in_=tot[:].rearrange("p c h -> p h c"), axis=mybir.AxisListType.X, op=ALU.add),
                 reads=["tot"], writes=["lt"])
            P.dma(lambda e: e.dma_start(out=send_d, in_=St[:]), reads=["St"], writes=["o_send"])
            P.dma(lambda e: e.dma_start(out=ltot_d, in_=lt[:]), reads=["lt"], writes=["o_lt"])


def stage_lru(P, S, full, xle_d, gT_d, pp, ppres, wabd, wxbd, wres, hprev_d, aprev_d, lruT, hend_d, atot_d):
    nsp8 = S.sb([128, 4], F32, "nsp8")
    hin = S.sb([128, 4], F32, "hin")
    hend = S.sb([128, 4], F32, "hend")
    rsum = S.sb([128, 4], F32, "rsum")
    P.op("act", lambda e: e.activation(out=nsp8[:], in_=pp["lam"], func=AF.Exp, scale=-1.0), reads=[ppres], writes=["nsp8"])
    P.op("act", lambda e: e.activation(out=nsp8[:], in_=nsp8[:], func=AF.Ln, bias=1.0), reads=["nsp8"], writes=["nsp8"])
    P.op("dve", lambda e: e.tensor_scalar(out=nsp8[:], in0=nsp8[:], scalar1=-8.0, scalar2=None, op0=ALU.mult), reads=["nsp8"], writes=["nsp8"])
    P.op("dve", lambda e: e.memset(hin[:], 0.0), writes=["hin"])
    if full:
        hp = S.sb([128, 3, 4], F32, "hp")
        ap_ = S.sb([128, 3, 4], F32, "ap")
        P.dma(lambda e: e.dma_start(out=hp[:], in_=hprev_d.rearrange("s p n -> p s n")), writes=["hp"])
        P.dma(lambda e: e.dma_start(out=ap_[:], in_=aprev_d.rearrange("s p n -> p s n")), writes=["ap"])
        P.op("act", lambda e: e.activation(out=ap_[:], in_=ap_[:], func=AF.Exp), reads=["ap"], writes=["ap"])
        for s_ in range(3):
            P.op("dve", lambda e: e.tensor_tensor(out=hin[:], in0=hin[:], in1=ap_[:, s_, :], op=ALU.mult), reads=["hin", "ap"], writes=["hin"])
            P.op("dve", lambda e: e.tensor_tensor(out=hin[:], in0=hin[:], in1=hp[:, s_, :], op=ALU.add), reads=["hin", "hp"], writes=["hin"])
    else:
        P.op("dve", lambda e: e.memset(rsum[:], 0.0), writes=["rsum"])
    bufs = {"ext": [S.sb([128, T + 3], F32, "lext") for _ in range(2)], "acc": [S.sb([128, T], F32, "lacc") for _ in range(2)]}
    xcb = S.sb([128, T], BF16, "xcb")
    rr = S.sb([128, T], F32, "rr")
    ii = S.sb([128, T], F32, "ii")
    aa = S.sb([128, T], F32, "aa")
    t1 = S.sb([128, T], F32, "t1")
    hh = S.sb([128, T], F32, "hh")
    if full:
        gg = [S.sb([128, T], F32, "gg") for _ in range(2)]
        t2 = S.sb([128, T], F32, "t2")
    psr = [S.ps([128, 512], F32, "psr") for _ in range(4)]
    np_ = 0
    for cc in range(4):
        if full:
            gb = cc % 2
            P.dma(lambda e: e.dma_start(out=gg[gb][:], in_=gT_d[:, cc, :]), writes=["gg%d" % gb])
        xc, xres = conv_fm(P, S, bufs, xle_d, cc, pp["lcw"], pp["lcb"], ppres, cc)
        P.op("act", lambda e: e.copy(out=xcb[:], in_=xc[:]), reads=[xres], writes=["xcb"])
        for (wt, bias, dst, dres) in ((wabd, pp["lba"], rr, "rr"), (wxbd, pp["lbx"], ii, "ii")):
            for tb in range(4):
                pb = np_ % 4
                np_ += 1
                ts = slice(tb * 512, (tb + 1) * 512)
                P.op("pe", lambda e: e.matmul(psr[pb][:], lhsT=wt[:, cc, :], rhs=xcb[:, ts], start=True, stop=True), reads=[wres, "xcb"], writes=["psr%d" % pb])
                if full or dres == "ii":
                    P.op("act", lambda e: e.activation(out=dst[:, ts], in_=psr[pb][:], func=AF.Sigmoid, bias=bias[:, cc:cc + 1]),
                         reads=["psr%d" % pb, ppres], writes=[dres + "%d" % tb])
                else:
                    P.op("act", lambda e: e.activation(out=dst[:, ts], in_=psr[pb][:], func=AF.Sigmoid, bias=bias[:, cc:cc + 1]),
                         reads=["psr%d" % pb, ppres], writes=[dres + "%d" % tb])
        rres = ["rr%d" % tb for tb in range(4)]
        ires = ["ii%d" % tb for tb in range(4)]
        if not full:
            P.op("dve", lambda e: e.reduce_sum(out=rsum[:, cc:cc + 1], in_=rr[:], axis=mybir.AxisListType.X), reads=rres + ["rsum"], writes=["rsum%d" % cc])
        P.op("act", lambda e: e.activation(out=aa[:], in_=rr[:], func=AF.Exp, scale=nsp8[:, cc:cc + 1]), reads=rres + ["nsp8"], writes=["aa"])
        P.op("pool", lambda e: e.tensor_tensor(out=t1[:], in0=aa[:], in1=aa[:], op=ALU.mult), reads=["aa"], writes=["t1"])
        P.op("pool", lambda e: e.tensor_scalar(out=t1[:], in0=t1[:], scalar1=-1.0, scalar2=1.0, op0=ALU.mult, op1=ALU.add), reads=["t1"], writes=["t1"])
        P.op("act", lambda e: e.activation(out=t1[:], in_=t1[:], func=AF.Sqrt), reads=["t1"], writes=["t1"])
        P.op("dve", lambda e: e.tensor_tensor(out=ii[:], in0=ii[:], in1=xc[:], op=ALU.mult), reads=ires + [xres], writes=["ii"])
        P.op("dve", lambda e: e.tensor_tensor(out=t1[:], in0=t1[:], in1=ii[:], op=ALU.mult), reads=["t1", "ii"], writes=["t1"])
        P.op("dve", lambda e: e.tensor_tensor_scan(out=hh[:], data0=aa[:], data1=t1[:], initial=hin[:, cc:cc + 1], op0=ALU.mult, op1=ALU.add),
             reads=["aa", "t1", "hin"], writes=["hh"])
        if full:
            g = gg[gb]
            P.op("pool", lambda e: e.tensor_tensor(out=t2[:], in0=g[:], in1=g[:], op=ALU.mult), reads=["gg%d" % gb], writes=["t2"])
            P.op("pool", lambda e: e.tensor_scalar(out=t2[:], in0=t2[:], scalar1=0.044715, scalar2=1.0, op0=ALU.mult, op1=ALU.add), reads=["t2"], writes=["t2"])
            P.op("pool", lambda e: e.tensor_tensor(out=t2[:], in0=t2[:], in1=g[:], op=ALU.mult), reads=["t2", "gg%d" % gb], writes=["t2"])
            P.op("act", lambda e: e.activation(out=t2[:], in_=t2[:], func=AF.Sigmoid, scale=1.5957691216057308), reads=["t2"], writes=["t2"])
            P.op("dve", lambda e: e.tensor_tensor(out=t2[:], in0=t2[:], in1=g[:], op=ALU.mult), reads=["t2", "gg%d" % gb], writes=["t2"])
            P.op("dve", lambda e: e.tensor_tensor(out=lruT[:, cc, :], in0=hh[:], in1=t2[:], op=ALU.mult), reads=["hh", "t2"], writes=["lruT%d" % cc])
        else:
            P.op("act", lambda e: e.copy(out=hend[:, cc:cc + 1], in_=hh[:, T - 1:T]), reads=["hh"], writes=["hend%d" % cc])
    if not full:
        P.op("dve", lambda e: e.tensor_tensor(out=rsum[:], in0=rsum[:], in1=nsp8[:], op=ALU.mult), reads=["rsum%d" % c for c in range(4)] + ["nsp8"], writes=["rsum"])
        P.dma(lambda e: e.dma_start(out=hend_d, in_=hend[:]), reads=["hend%d" % c for c in range(4)], writes=["o_hend"])
        P.dma(lambda e: e.dma_start(out=atot_d, in_=rsum[:]), reads=["rsum"], writes=["o_atot"])


PP_FIELDS = (("alog", 8), ("dtb", 8), ("dsk", 512), ("snw", 512), ("scw", 32), ("scb", 8), ("lcw", 16), ("lcb", 4),
             ("lba", 4), ("lbx", 4), ("lam", 4), ("wa", 512), ("wx", 512), ("gff", 8), ("nfin", 1024))
PP_OFF = {}
_o = 0
for _n, _w in PP_FIELDS:
    PP_OFF[_n] = (_o, _w)
    _o += _w
PP_W = _o
CF_FIELDS = (("U", 128), ("onesf", 128), ("identf", 128), ("maskneg4", 512), ("lstrict3", 1024))
CF_OFF = {}
_o = 0
for _n, _w in CF_FIELDS:
    CF_OFF[_n] = (_o, _w)
    _o += _w
CF_W = _o


def host_pp(inp, l):
    f = np.float32
    pp = np.zeros((128, PP_W), f)

    def put(name, arr):
        o, w = PP_OFF[name]
        pp[:, o:o + w] = np.asarray(arr, f).reshape(128, w) if np.asarray(arr).size == 128 * w else np.broadcast_to(np.asarray(arr, f).reshape(1, w), (128, w))

    put("alog", inp["ssd_a_log"][l])
    put("dtb", inp["ssd_dt_bias"][l])
    put("dsk", np.repeat(inp["ssd_d"][l], 64))
    put("snw", inp["ssd_norm"][l])
    put("scw", inp["ssd_conv_w"][l].T.reshape(8, 128, 4).transpose(1, 0, 2).reshape(128, 32))
    put("scb", inp["ssd_conv_b"][l].reshape(8, 128).T)
    put("lcw", inp["lru_conv_w"][l].T.reshape(4, 128, 4).transpose(1, 0, 2).reshape(128, 16))
    put("lcb", inp["lru_conv_b"][l].reshape(4, 128).T)
    put("lba", inp["lru_ba"][l].reshape(4, 128).T)
    put("lbx", inp["lru_bx"][l].reshape(4, 128).T)
    put("lam", inp["lru_lambda"][l].reshape(4, 128).T)
    for nm, w in (("wa", inp["lru_wa"][l]), ("wx", inp["lru_wx"][l])):
        bd = np.zeros((128, 4, 128), f)
        for c in range(4):
            bd[0:64, c, 0:64] = w[2 * c]
            bd[64:128, c, 64:128] = w[2 * c + 1]
        put(nm, bd.reshape(128, 512))
    put("gff", inp["norm_ffn"][l].reshape(8, 128).T)
    put("nfin", inp["norm_final"])
    return pp


def host_cf():
    f = np.float32
    cfm = np.zeros((128, CF_W), f)
    k = np.arange(128)[:, None]
    j = np.arange(128)[None, :]
    cfm[:, 0:128] = (k <= j)
    cfm[:, 128:256] = 1.0
    cfm[:, 256:384] = (k == j)
    mneg = np.where(j >= k, 0.0, NEG).astype(f)
    cfm[:, 384:896] = np.tile(mneg, (1, 4))
    cfm[:, 896:1920] = np.tile((k > j).astype(f), (1, 8))
    return cfm


def host_base(q):
    f = np.float32
    base = np.zeros((3, 128, 4, 128), f)
    jj = np.arange(128)[:, None]
    ii = np.arange(128)[None, :]
    big = -1.0e9
    for di, d in enumerate(DILS):
        prev = np.where(jj >= ii, -float(d) * (128 + ii - jj), big)
        cur = np.where(jj <= ii, -float(d) * (ii - jj), big)
        base[di, :, 0] = prev
        base[di, :, 1] = cur
        base[di, :, 2] = prev if q > 0 else big
        base[di, :, 3] = cur
    return base


def pp_views(ppt):
    v = {}
    for n, (o, w) in PP_OFF.items():
        v[n] = ppt[:, o:o + w]
    v["scw"] = v["scw"].rearrange("p (c k) -> p c k", k=4)
    v["lcw"] = v["lcw"].rearrange("p (c k) -> p c k", k=4)
    return v


def cf_views(cft):
    v = {}
    for n, (o, w) in CF_OFF.items():
        v[n] = cft[:, o:o + w]
    v["lstrict3"] = v["lstrict3"].rearrange("p (h j) -> p h j", h=8)
    return v


def load_packs(P, S, pp_d, cf_d):
    ppt = S.sb([128, PP_W], F32, "ppt")
    cft = S.sb([128, CF_W], F32, "cft")
    P.dma(lambda e: e.dma_start(out=ppt[:], in_=pp_d), writes=["pp"])
    P.dma(lambda e: e.dma_start(out=cft[:], in_=cf_d), writes=["cf"])
    wabd = S.sb([128, 4, 128], BF16, "wabd")
    wxbd = S.sb([128, 4, 128], BF16, "wxbd")
    pv = pp_views(ppt)
    P.op("dve", lambda e: e.tensor_copy(out=wabd[:].rearrange("p c m -> p (c m)"), in_=pv["wa"]), reads=["pp"], writes=["wbd"])
    P.op("dve", lambda e: e.tensor_copy(out=wxbd[:].rearrange("p c m -> p (c m)"), in_=pv["wx"]), reads=["pp"], writes=["wbd"])
    return pv, cf_views(cft), wabd, wxbd


def build_LB():
    nc = new_nc()
    xbce = dram_in(nc, "xbce", [128, 8, T + 3], F32)
    xle = dram_in(nc, "xle", [128, 4, T + 3], F32)
    dtr = dram_in(nc, "dtr", [T, 64], F32)
    pp_d = dram_in(nc, "pp", [128, PP_W], F32)
    cf_d = dram_in(nc, "cf", [128, CF_W], F32)
    send = dram_out(nc, "send", [128, 512], F32)
    ltot = dram_out(nc, "ltot", [128, 8], F32)
    hend = dram_out(nc, "hend", [128, 4], F32)
    atot = dram_out(nc, "atot", [128, 4], F32)
    with ExitStack() as st:
        P = Prog(nc, st)
        with P.scope() as S0:
            pv, cv, wabd, wxbd = load_packs(P, S0, pp_d, cf_d)
            ident = make_ident(P, S0)
            with P.scope() as S:
                stage_ssd(P, S, False, xbce, dtr, None, pv, "pp", cv, "cf", ident, None, None, None, send, ltot)
            with P.scope() as S:
                stage_lru(P, S, False, xle, None, pv, "pp", wabd, wxbd, "wbd", None, None, None, hend, atot)
            P.barrier()
    return nc


def stage_outproj(P, S, x_sb, yT_list, wout_d):
    ws = WStream(P, S, 12, 256, 2, "wo", nstg=1)
    ps = [S.ps([128, 512], F32, "pso") for _ in range(4)]
    n = 0
    for q4 in range(4):
        wbf, wres = ws.load(wout_d, 0, q4 * 256)
        cs = slice(q4 * 256, (q4 + 1) * 256)
        for t in range(NT):
            pb = n % 4
            n += 1
            for kc in range(12):
                yT = yT_list[kc // 4]
                P.op("pe", lambda e: e.matmul(ps[pb][:, :256], lhsT=yT[:, kc % 4, t * 128:(t + 1) * 128], rhs=wbf[:, kc, :], start=(kc == 0), stop=(kc == 11)),
                     reads=[wres[kc]], writes=["pso%d" % pb])
            P.op("dve", lambda e: e.tensor_tensor(out=x_sb[:, t, cs], in0=x_sb[:, t, cs], in1=ps[pb][:, :256], op=ALU.add),
                 reads=["pso%d" % pb, "x%d" % t], writes=["x%d" % t])


def stage_ffn(P, S, x_sb, h2T, gain, gres, wg_d, wu_d, wd_d):
    FG = 2
    wsg = WStream(P, S, KD, FG * 128, 2, "wg", nstg=1)
    wsu = WStream(P, S, KD, FG * 128, 2, "wu", nstg=1)
    wsd = WStream(P, S, FG, 1024, 2, "wd", nstg=1)
    ps_g = [S.ps([128, 512], F32, "psg") for _ in range(2)]
    ps_u = [S.ps([128, 512], F32, "psu") for _ in range(2)]
    ps_d = [S.ps([128, 512], F32, "psd") for _ in range(2)]
    sg = [S.sb([128, 512], F32, "sg") for _ in range(2)]
    actT = [S.sb([128, FG, 512], BF16, "actT") for _ in range(2)]
    hres = ["h2T%d" % t for t in range(NT)]
    n = 0
    nd = 0
    na = 0
    for f0 in range(0, NFC, FG):
        nf = min(FG, NFC - f0)
        wg, wgres = wsg.load(wg_d, 0, f0 * 128, ncols=nf * 128, gain=gain, gain_res=gres)
        wu, wures = wsu.load(wu_d, 0, f0 * 128, ncols=nf * 128, gain=gain, gain_res=gres)
        wd, wdres = wsd.load(wd_d, f0 * 128, 0, kc=nf)
        for tb in range(4):
            ab = na % 2
            na += 1
            ts = slice(tb * 512, (tb + 1) * 512)
            for fc in range(nf):
                pb = n % 2
                n += 1
                for k in range(KD):
                    P.op("pe", lambda e: e.matmul(ps_g[pb][:], lhsT=wg[:, k, fc * 128:(fc + 1) * 128], rhs=h2T[:, k, ts], start=(k == 0), stop=(k == KD - 1)),
                         reads=[wgres[k]] + hres[4 * tb:4 * tb + 4], writes=["psg%d" % pb])
                for k in range(KD):
                    P.op("pe", lambda e: e.matmul(ps_u[pb][:], lhsT=wu[:, k, fc * 128:(fc + 1) * 128], rhs=h2T[:, k, ts], start=(k == 0), stop=(k == KD - 1)),
                         reads=[wures[k]] + hres[4 * tb:4 * tb + 4], writes=["psu%d" % pb])
                P.op("act", lambda e: e.activation(out=sg[pb][:], in_=ps_g[pb][:], func=AF.Silu), reads=["psg%d" % pb], writes=["sg%d" % pb])
                P.op("dve", lambda e: e.tensor_tensor(out=actT[ab][:, fc, :], in0=sg[pb][:], in1=ps_u[pb][:], op=ALU.mult),
                     reads=["sg%d" % pb, "psu%d" % pb], writes=["actT%d_%d" % (ab, fc)])
            for tl in range(4):
                t = tb * 4 + tl
                for half in range(2):
                    db = nd % 2
                    nd += 1
                    for fc in range(nf):
                        P.op("pe", lambda e: e.matmul(ps_d[db][:], lhsT=actT[ab][:, fc, tl * 128:(tl + 1) * 128], rhs=wd[:, fc, half * 512:(half + 1) * 512],
                                                       start=(fc == 0), stop=(fc == nf - 1)),
                             reads=["actT%d_%d" % (ab, fc), wdres[fc]], writes=["psd%d" % db])
                    eng = "dve" if half == 0 else "pool"
                    if eng == "pool":
                        P.op("act", lambda e: e.copy(out=sg[db][:], in_=ps_d[db][:]), reads=["psd%d" % db], writes=["sg%d" % db])
                        P.op("pool", lambda e: e.tensor_tensor(out=x_sb[:, t, half * 512:(half + 1) * 512], in0=x_sb[:, t, half * 512:(half + 1) * 512], in1=sg[db][:], op=ALU.add),
                             reads=["sg%d" % db, "x%d" % t], writes=["x%d" % t])
                    else:
                        P.op("dve", lambda e: e.tensor_tensor(out=x_sb[:, t, half * 512:(half + 1) * 512], in0=x_sb[:, t, half * 512:(half + 1) * 512], in1=ps_d[db][:], op=ALU.add),
                             reads=["psd%d" % db, "x%d" % t], writes=["x%d" % t])


def stage_final_norm(P, S, x_sb, nfin, nres, out_d):
    ssq = S.sb([128, NT], F32, "fssq")
    junk = S.sb([128, D], BF16, "fjunk")
    ot = [S.sb([128, D], F32, "fot") for _ in range(2)]
    P.op("dve", lambda e: e.memset(ssq[:], 0.0), writes=["fssq"])
    for t in range(NT):
        P.op("act", lambda e: e.activation(out=junk[:], in_=x_sb[:, t, :], func=AF.Square, accum_out=ssq[:, t:t + 1]),
             reads=["x%d" % t, "fssq"], writes=["fjunk", "fssq%d" % t])
    P.op("act", lambda e: e.activation(out=ssq[:], in_=ssq[:], func=AF.Sqrt, bias=1e-6, scale=1.0 / D), reads=["fssq%d" % t for t in range(NT)], writes=["fstd"])
    P.op("dve", lambda e: e.reciprocal(out=ssq[:], in_=ssq[:]), reads=["fstd"], writes=["frstd"])
    ov = out_d.rearrange("(n p) d -> p n d", p=128)
    for t in range(NT):
        b = t % 2
        P.op("dve", lambda e: e.scalar_tensor_tensor(out=ot[b][:], in0=x_sb[:, t, :], scalar=ssq[:, t:t + 1], in1=nfin, op0=ALU.mult, op1=ALU.mult),
             reads=["x%d" % t, "frstd", nres], writes=["fot%d" % b])
        P.dma(lambda e: e.dma_start(out=ov[:, t, :], in_=ot[b][:]), reads=["fot%d" % b], writes=["o_final"])


def build_LC(last, upto=99):
    nc = new_nc()
    dbgm = upto < 99
    x = None if dbgm else dram_in(nc, "x", [T, D], F32)
    qT = dram_in(nc, "qT", [128, 4, T], BF16)
    kTe = dram_in(nc, "kTe", [128, 4, 2 * T], BF16)
    vE = [dram_in(nc, "vE%d" % di, [4, 128, nu, 128], BF16) for di, nu in enumerate((17, 20, 32))]
    base = dram_in(nc, "base", [3, 128, 4, 128], F32)
    z = dram_in(nc, "z", [T, 512], F32)
    dtr = dram_in(nc, "dtr", [T, 64], F32)
    xbce = dram_in(nc, "xbce", [128, 8, T + 3], F32)
    xle = dram_in(nc, "xle", [128, 4, T + 3], F32)
    gT = dram_in(nc, "gT", [128, 4, T], F32)
    pp_d = dram_in(nc, "pp", [128, PP_W], F32)
    cf_d = dram_in(nc, "cf", [128, CF_W], F32)
    sprev = dram_in(nc, "sprev", [3, 128, 512], F32)
    lprev = dram_in(nc, "lprev", [3, 128, 8], F32)
    hprev = dram_in(nc, "hprev", [3, 128, 4], F32)
    aprev = dram_in(nc, "aprev", [3, 128, 4], F32)
    if not dbgm:
        w_out = dram_in(nc, "w_out", [1536, D], F32)
        w_gate = dram_in(nc, "w_gate", [D, DFF], F32)
        w_up = dram_in(nc, "w_up", [D, DFF], F32)
        w_down = dram_in(nc, "w_down", [DFF, D], F32)
        xo = dram_out(nc, "xo", [T, D], F32)
    dbg = dram_out(nc, "dbg", [3, 128, 4, T], BF16) if upto < 99 else None
    with ExitStack() as st:
        P = Prog(nc, st)
        with P.scope() as S0:
            pv, cv, wabd, wxbd = load_packs(P, S0, pp_d, cf_d)
            ident = make_ident(P, S0)
            with P.scope() as SM:
                attT = SM.sb([128, 4, T], BF16, "attT")
                ssdT = SM.sb([128, 4, T], BF16, "ssdT")
                lruT = SM.sb([128, 4, T], BF16, "lruT")
                if upto >= 1:
                    with P.scope() as S:
                        stage_attention(P, S, qT, kTe, vE, base, attT)
                if upto >= 2:
                    with P.scope() as S:
                        stage_ssd(P, S, True, xbce, dtr, z, pv, "pp", cv, "cf", ident, sprev, lprev, ssdT, None, None)
                if upto >= 3:
                    with P.scope() as S:
                        stage_lru(P, S, True, xle, gT, pv, "pp", wabd, wxbd, "wbd", hprev, aprev, lruT, None, None)
                if dbg is not None:
                    for i, tt in enumerate((attT, ssdT, lruT)):
                        P.dma(lambda e: e.dma_start(out=dbg[i], in_=tt[:]), writes=["o_dbg"])
                    P.barrier()
                    return nc
                with P.scope() as S1:
                    x_sb = S1.sb([128, NT, D], F32, "x")
                    load_x(P, x, x_sb)
                    with P.scope() as S:
                        stage_outproj(P, S, x_sb, [attT, ssdT, lruT], w_out)
                    xv = xo.rearrange("(n p) d -> p n d", p=128)
                    for i in range(4):
                        P.dma(lambda e: e.dma_start(out=xv[:, 4 * i:4 * i + 4, :], in_=x_sb[:, 4 * i:4 * i + 4, :]),
                              reads=["x%d" % t for t in range(4 * i, 4 * i + 4)], writes=["o_x%d" % i])
            with P.scope() as S1:
                x_sb = S1.sb([128, NT, D], F32, "x")
                load_x(P, xo, x_sb)
                h2T = S1.sb([128, KD, T], BF16, "h2T")
                with P.scope() as S:
                    stage_norm_T(P, S, x_sb, h2T, ident)
                with P.scope() as S:
                    stage_ffn(P, S, x_sb, h2T, pv["gff"], "pp", w_gate, w_up, w_down)
                with P.scope() as S:
                    if last:
                        stage_final_norm(P, S, x_sb, pv["nfin"], "pp", xo)
                    else:
                        xv = xo.rearrange("(n p) d -> p n d", p=128)
                        for i in range(4):
                            P.dma(lambda e: e.dma_start(out=xv[:, 4 * i:4 * i + 4, :], in_=x_sb[:, 4 * i:4 * i + 4, :]), writes=["o_x"])
            P.barrier()
    return nc


_NC_CACHE = {}


def _get_nc(key):
    if key not in _NC_CACHE:
        if key == "A":
            _NC_CACHE[key] = build_LA()
        elif key == "B":
            _NC_CACHE[key] = build_LB()
        elif key == "C0":
            _NC_CACHE[key] = build_LC(False)
        else:
            _NC_CACHE[key] = build_LC(True)
    return _NC_CACHE[key]


def _run(key, maps):
    res = run_bass_kernel_spmd(_get_nc(key), maps, core_ids=list(range(8)))
    return res.results


def glue_A(rA):
    outs = []
    for c in range(8):
        q = c % 4
        own = rA[c]
        prev = rA[c - 1] if q > 0 else None
        kT = np.asarray(own["kT"])
        kprev = np.asarray(prev["kT"]) if prev is not None else np.zeros_like(kT)
        kTe = np.concatenate([kprev, kT], axis=2)
        vU = np.asarray(own["vU"])
        vP = np.asarray(prev["vU"]) if prev is not None else np.zeros_like(vU)
        vE0 = np.concatenate([vP[0][15:16], vU[0]], axis=0)
        vE1 = np.concatenate([np.concatenate([vP[1][r * 4 + 3:r * 4 + 4], vU[1][r * 4:r * 4 + 4]], axis=0) for r in range(4)], axis=0)
        vE2 = np.stack([vP[2], vU[2]], axis=1).reshape(32, 128, 512)
        vE = [np.ascontiguousarray(v.reshape(v.shape[0], 128, 4, 128).transpose(2, 1, 0, 3)) for v in (vE0, vE1, vE2)]
        xb = np.asarray(own["xbcT"])
        xbp = np.asarray(prev["xbcT"])[:, :, -3:] if prev is not None else np.zeros((128, 8, 3), np.float32)
        xl = np.asarray(own["xlT"])
        xlp = np.asarray(prev["xlT"])[:, :, -3:] if prev is not None else np.zeros((128, 4, 3), np.float32)
        outs.append({"qT": np.asarray(own["qT"]), "kTe": kTe, "vE0": vE[0], "vE1": vE[1], "vE2": vE[2], "z": np.asarray(own["z"]),
                     "dtr": np.asarray(own["dtr"]), "xbce": np.concatenate([xbp, xb], axis=2), "xle": np.concatenate([xlp, xl], axis=2),
                     "gT": np.asarray(own["gT"])})
    return outs


def glue_B(rB):
    outs = []
    for c in range(8):
        b, q = c // 4, c % 4
        d = {"sprev": np.zeros((3, 128, 512), np.float32), "lprev": np.zeros((3, 128, 8), np.float32),
             "hprev": np.zeros((3, 128, 4), np.float32), "aprev": np.zeros((3, 128, 4), np.float32)}
        for s in range(3):
            qq = q - 3 + s
            if qq >= 0:
                r = rB[b * 4 + qq]
                d["sprev"][s] = r["send"]
                d["lprev"][s] = r["ltot"]
                d["hprev"][s] = r["hend"]
                d["aprev"][s] = r["atot"]
        outs.append(d)
    return outs


def kernel(**inp):
    inp = {k: np.asarray(v) for k, v in inp.items()}
    x = inp["x"]
    xs = [np.ascontiguousarray(x[c // 4, (c % 4) * T:(c % 4 + 1) * T]) for c in range(8)]
    cfm = host_cf()
    bases = [host_base(c % 4) for c in range(8)]
    for l in range(2):
        w_in = np.ascontiguousarray(inp["w_in"][l])
        gmix = np.ascontiguousarray(inp["norm_mix"][l].reshape(8, 128).T)
        rA = _run("A", [{"x": xs[c], "w_in": w_in, "gmix": gmix} for c in range(8)])
        gA = glue_A(rA)
        pp = host_pp(inp, l)
        rB = _run("B", [{"xbce": gA[c]["xbce"], "xle": gA[c]["xle"], "dtr": gA[c]["dtr"], "pp": pp, "cf": cfm} for c in range(8)])
        gB = glue_B(rB)
        maps = []
        for c in range(8):
            m = {"x": xs[c], "base": bases[c], "pp": pp, "cf": cfm, "w_out": np.ascontiguousarray(inp["w_out"][l]),
                 "w_gate": np.ascontiguousarray(inp["w_gate"][l]), "w_up": np.ascontiguousarray(inp["w_up"][l]),
                 "w_down": np.ascontiguousarray(inp["w_down"][l])}
            m.update(gA[c])
            m.update(gB[c])
            maps.append(m)
        rC = _run("C1" if l == 1 else "C0", maps)
        xs = [np.asarray(rC[c]["xo"]) for c in range(8)]
    out = np.zeros((2, 4 * T, D), np.float32)
    for c in range(8):
        out[c // 4, (c % 4) * T:(c % 4 + 1) * T] = xs[c]
    return out


def build_dbg_ssd():
    nc = new_nc()
    z = dram_in(nc, "z", [T, 512], F32)
    dtr = dram_in(nc, "dtr", [T, 64], F32)
    xbce = dram_in(nc, "xbce", [128, 8, T + 3], F32)
    pp_d = dram_in(nc, "pp", [128, PP_W], F32)
    cf_d = dram_in(nc, "cf", [128, CF_W], F32)
    sprev = dram_in(nc, "sprev", [3, 128, 512], F32)
    lprev = dram_in(nc, "lprev", [3, 128, 8], F32)
    dbg = {"y0": dram_out(nc, "y0", [T, 512], F32), "L": dram_out(nc, "L", [128, 512], F32), "xs": dram_out(nc, "xs", [128, 512], BF16),
           "sm": dram_out(nc, "sm", [3, 128, 8], F32)}
    o = dram_out(nc, "ssdT", [128, 4, T], BF16)
    with ExitStack() as st:
        P = Prog(nc, st)
        with P.scope() as S0:
            pv, cv, wabd, wxbd = load_packs(P, S0, pp_d, cf_d)
            ident = make_ident(P, S0)
            ssdT = S0.sb([128, 4, T], BF16, "ssdT")
            with P.scope() as S:
                stage_ssd(P, S, True, xbce, dtr, z, pv, "pp", cv, "cf", ident, sprev, lprev, ssdT, None, None, dbg=dbg)
            P.dma(lambda e: e.dma_start(out=o, in_=ssdT[:]), writes=["o"])
            P.barrier()
    return nc
```
